# Optimizing a Trainium2 kernel written in Bass

```python
import math
import jax, jax.numpy as jnp
from jax import lax
import numpy as np


D_MODEL = 1024
BATCH = 16
SEQ = 4096
DEPTH = 1

MIX_WIDTH = D_MODEL
SSD_WIDTH = MIX_WIDTH // 2
SSD_HEAD_DIM = 64
SSD_HEADS = SSD_WIDTH // SSD_HEAD_DIM
SSD_GROUPS = 2
SSD_STATE = 128
SSD_CONV = 4
SSD_CHUNK = 128
SSD_BC = SSD_GROUPS * SSD_STATE
CONV_CH = SSD_WIDTH + 2 * SSD_BC
S5_WIDTH = MIX_WIDTH - SSD_WIDTH
S5_GROUP_CH = 16
S5_GROUPS = S5_WIDTH // S5_GROUP_CH
S5_STATE = 64
IN_COLS = SSD_WIDTH + CONV_CH + SSD_HEADS + S5_WIDTH
D_FF = ((8 * D_MODEL // 3 + 255) // 256) * 256
N_MOD = 9
ALPHA = (2 * DEPTH) ** 0.25
BETA = (8 * DEPTH) ** -0.25
LN_EPS = 1e-5

kernel_name = 'hymba_ssd_s5_macaron_deepnorm'


def layer_norm(x, g, b):
    xf = x.astype(jnp.float32)
    mu = jnp.mean(xf, axis=-1, keepdims=True)
    var = jnp.mean(jnp.square(xf - mu), axis=-1, keepdims=True)
    return ((xf - mu) * lax.rsqrt(var + LN_EPS) * g + b).astype(x.dtype)


def modulate(x, shift, scale):
    return x * (1 + scale[:, None, :]) + shift[:, None, :]


def swiglu(u, w1, w3, w2):
    return (jax.nn.silu(u @ w1) * (u @ w3)) @ w2


def causal_dwconv(x, w, b):
    k = w.shape[0]
    y = lax.conv_general_dilated(x, w[:, None, :], window_strides=(1,), padding=[(k - 1, 0)],
                                 dimension_numbers=('NWC', 'WIO', 'NWC'),
                                 feature_group_count=x.shape[-1])
    return y + b


def ssd_chunked(xs, dt, a, bm, cm):
    bsz, s_len, n_h, p = xs.shape
    n_g, n_s = bm.shape[-2:]
    n_z = n_h // n_g
    l = SSD_CHUNK
    nc = s_len // l
    x = (xs * dt[..., None]).reshape(bsz, nc, l, n_g, n_z, p)
    a_dt = (dt * a).reshape(bsz, nc, l, n_g, n_z).transpose(0, 3, 4, 1, 2)
    bm = bm.reshape(bsz, nc, l, n_g, n_s)
    cm = cm.reshape(bsz, nc, l, n_g, n_s)
    a_cs = jnp.cumsum(a_dt, axis=-1)
    causal = jnp.tril(jnp.ones((l, l), dtype=bool))
    seg = a_cs[..., :, None] - a_cs[..., None, :]
    lmat = jnp.exp(jnp.where(causal, seg, -jnp.inf))
    cb = jnp.einsum('bclgn,bcsgn->bcgls', cm, bm)
    y_diag = jnp.einsum('bcgls,bgzcls,bcsgzp->bclgzp', cb, lmat, x)
    decay = jnp.exp(a_cs[..., -1:] - a_cs)
    states = jnp.einsum('bclgn,bgzcl,bclgzp->bcgzpn', bm, decay, x)
    chunk_decay = jnp.exp(a_cs[..., -1])

    def step(h, inp):
        s_c, d_c = inp
        return d_c[..., None, None] * h + s_c, h

    h0 = jnp.zeros((bsz, n_g, n_z, p, n_s), dtype=states.dtype)
    _, prev = lax.scan(step, h0, (jnp.moveaxis(states, 1, 0), jnp.moveaxis(chunk_decay, 3, 0)))
    prev = jnp.moveaxis(prev, 0, 1)
    y_off = jnp.einsum('bclgn,bcgzpn,bgzcl->bclgzp', cm, prev, jnp.exp(a_cs))
    return (y_diag + y_off).reshape(bsz, s_len, n_h, p)


def s5_mixer(u, a_re, a_im, log_dt, b_re, b_im, c_re, c_im, d, w_glu, b_glu):
    f32 = jnp.float32
    bsz, s_len, _ = u.shape
    uf = u.astype(f32).reshape(bsz, s_len, S5_GROUPS, S5_GROUP_CH)
    ar, ai = a_re.astype(f32), a_im.astype(f32)
    dt = jnp.exp(log_dt.astype(f32))[:, None]
    mag = jnp.exp(dt * ar)
    ab_re, ab_im = mag * jnp.cos(dt * ai), mag * jnp.sin(dt * ai)
    den = ar * ar + ai * ai
    nr, ni = ab_re - 1.0, ab_im
    f_re, f_im = (nr * ar + ni * ai) / den, (ni * ar - nr * ai) / den
    br, bi = b_re.astype(f32), b_im.astype(f32)
    bb_re = f_re[..., None] * br - f_im[..., None] * bi
    bb_im = f_re[..., None] * bi + f_im[..., None] * br
    bu_re = jnp.einsum('bsgh,gph->bsgp', uf, bb_re)
    bu_im = jnp.einsum('bsgh,gph->bsgp', uf, bb_im)
    a_seq_re = jnp.broadcast_to(ab_re, (1, s_len, S5_GROUPS, S5_STATE))
    a_seq_im = jnp.broadcast_to(ab_im, (1, s_len, S5_GROUPS, S5_STATE))

    def combine(e1, e2):
        a1r, a1i, b1r, b1i = e1
        a2r, a2i, b2r, b2i = e2
        return (a2r * a1r - a2i * a1i, a2r * a1i + a2i * a1r,
                a2r * b1r - a2i * b1i + b2r, a2r * b1i + a2i * b1r + b2i)

    _, _, xr, xi = lax.associative_scan(combine, (a_seq_re, a_seq_im, bu_re, bu_im), axis=1)
    y = (jnp.einsum('bsgp,ghp->bsgh', xr, c_re.astype(f32))
         - jnp.einsum('bsgp,ghp->bsgh', xi, c_im.astype(f32))
         + uf * d.astype(f32).reshape(S5_GROUPS, S5_GROUP_CH))
    y = y.reshape(bsz, s_len, S5_WIDTH)
    g = jax.nn.gelu(y)
    out = g * jax.nn.sigmoid(g @ w_glu.astype(f32) + b_glu.astype(f32))
    return out.astype(u.dtype)


def hybrid_mixer(h, w_in, conv_w, conv_b, dt_bias, a_log, d_ssd, ssd_norm_w,
                 s5_a_re, s5_a_im, s5_log_dt, s5_b_re, s5_b_im, s5_c_re, s5_c_im, s5_d,
                 w_glu, b_glu, w_out):
    f32 = jnp.float32
    bsz, s_len, _ = h.shape
    proj = h @ w_in
    z, xbc, dt_raw, u = jnp.split(proj, [SSD_WIDTH, SSD_WIDTH + CONV_CH,
                                         SSD_WIDTH + CONV_CH + SSD_HEADS], axis=-1)
    xbc = jax.nn.silu(causal_dwconv(xbc, conv_w, conv_b))
    xs, bm, cm = jnp.split(xbc.astype(f32), [SSD_WIDTH, SSD_WIDTH + SSD_BC], axis=-1)
    dt = jax.nn.softplus(dt_raw.astype(f32) + dt_bias.astype(f32))
    a = -jnp.exp(a_log.astype(f32))
    xs = xs.reshape(bsz, s_len, SSD_HEADS, SSD_HEAD_DIM)
    y = ssd_chunked(xs, dt, a,
                    bm.reshape(bsz, s_len, SSD_GROUPS, SSD_STATE),
                    cm.reshape(bsz, s_len, SSD_GROUPS, SSD_STATE))
    y = y + d_ssd.astype(f32)[:, None] * xs
    y = y.reshape(bsz, s_len, SSD_WIDTH) * jax.nn.silu(z.astype(f32))
    yg = y.reshape(bsz, s_len, SSD_GROUPS, SSD_WIDTH // SSD_GROUPS)
    yg = yg * lax.rsqrt(jnp.mean(jnp.square(yg), axis=-1, keepdims=True) + LN_EPS)
    y_ssd = (yg.reshape(bsz, s_len, SSD_WIDTH) * ssd_norm_w.astype(f32)).astype(h.dtype)
    y_s5 = s5_mixer(u, s5_a_re, s5_a_im, s5_log_dt, s5_b_re, s5_b_im, s5_c_re, s5_c_im,
                    s5_d, w_glu, b_glu)
    return jnp.concatenate([y_ssd, y_s5], axis=-1) @ w_out


def setup_inputs(seed: int = 0) -> dict:
    key = jax.random.key(seed)
    ks = iter(jax.random.split(key, 48))
    f32 = jnp.float32
    nl = DEPTH

    def nrm(shape, std):
        return std * jax.random.normal(next(ks), shape, f32)

    def unif(shape, lo, hi):
        return jax.random.uniform(next(ks), shape, f32, minval=lo, maxval=hi)

    x = nrm((BATCH, SEQ, D_MODEL), 1.0)
    c = nrm((BATCH, D_MODEL), 1.0)
    w_ada = nrm((nl, D_MODEL, N_MOD * D_MODEL), 0.5 * D_MODEL ** -0.5)
    b_ada = nrm((nl, N_MOD * D_MODEL), 0.02)
    ffn1_w1 = nrm((nl, D_MODEL, D_FF), D_MODEL ** -0.5)
    ffn1_w3 = nrm((nl, D_MODEL, D_FF), D_MODEL ** -0.5)
    ffn1_w2 = nrm((nl, D_FF, D_MODEL), BETA * D_FF ** -0.5)
    ln1_g = 1.0 + nrm((nl, D_MODEL), 0.02)
    ln1_b = nrm((nl, D_MODEL), 0.02)
    w_in = nrm((nl, D_MODEL, IN_COLS), D_MODEL ** -0.5)
    conv_w = nrm((nl, SSD_CONV, CONV_CH), SSD_CONV ** -0.5)
    conv_b = nrm((nl, CONV_CH), 0.01)
    dt0 = jnp.exp(unif((nl, SSD_HEADS), math.log(1e-3), math.log(1e-1)))
    dt_bias = dt0 + jnp.log(-jnp.expm1(-dt0))
    a_log = jnp.log(unif((nl, SSD_HEADS), 1.0, 16.0))
    d_ssd = 1.0 + nrm((nl, SSD_HEADS), 0.1)
    ssd_norm_w = 1.0 + nrm((nl, SSD_WIDTH), 0.02)
    s5_a_re = -0.5 + nrm((nl, S5_GROUPS, S5_STATE), 0.01)
    s5_a_im = math.pi * jnp.arange(S5_STATE, dtype=f32)[None, None, :] + nrm((nl, S5_GROUPS, S5_STATE), 0.01)
    s5_log_dt = unif((nl, S5_GROUPS), math.log(1e-3), math.log(1e-1))
    s5_b_re = nrm((nl, S5_GROUPS, S5_STATE, S5_GROUP_CH), (2 * S5_GROUP_CH) ** -0.5)
    s5_b_im = nrm((nl, S5_GROUPS, S5_STATE, S5_GROUP_CH), (2 * S5_GROUP_CH) ** -0.5)
    s5_c_re = nrm((nl, S5_GROUPS, S5_GROUP_CH, S5_STATE), S5_STATE ** -0.5)
    s5_c_im = nrm((nl, S5_GROUPS, S5_GROUP_CH, S5_STATE), S5_STATE ** -0.5)
    s5_d = nrm((nl, S5_WIDTH), 1.0)
    w_glu = nrm((nl, S5_WIDTH, S5_WIDTH), S5_WIDTH ** -0.5)
    b_glu = nrm((nl, S5_WIDTH), 0.01)
    w_out = nrm((nl, MIX_WIDTH, D_MODEL), BETA * MIX_WIDTH ** -0.5)
    ln2_g = 1.0 + nrm((nl, D_MODEL), 0.02)
    ln2_b = nrm((nl, D_MODEL), 0.02)
    ffn2_w1 = nrm((nl, D_MODEL, D_FF), D_MODEL ** -0.5)
    ffn2_w3 = nrm((nl, D_MODEL, D_FF), D_MODEL ** -0.5)
    ffn2_w2 = nrm((nl, D_FF, D_MODEL), BETA * D_FF ** -0.5)
    ln3_g = 1.0 + nrm((nl, D_MODEL), 0.02)
    ln3_b = nrm((nl, D_MODEL), 0.02)
    return {'x': x, 'c': c, 'w_ada': w_ada, 'b_ada': b_ada,
            'ffn1_w1': ffn1_w1, 'ffn1_w3': ffn1_w3, 'ffn1_w2': ffn1_w2, 'ln1_g': ln1_g, 'ln1_b': ln1_b,
            'w_in': w_in, 'conv_w': conv_w, 'conv_b': conv_b, 'dt_bias': dt_bias, 'a_log': a_log,
            'd_ssd': d_ssd, 'ssd_norm_w': ssd_norm_w, 's5_a_re': s5_a_re, 's5_a_im': s5_a_im,
            's5_log_dt': s5_log_dt, 's5_b_re': s5_b_re, 's5_b_im': s5_b_im, 's5_c_re': s5_c_re,
            's5_c_im': s5_c_im, 's5_d': s5_d, 'w_glu': w_glu, 'b_glu': b_glu, 'w_out': w_out,
            'ln2_g': ln2_g, 'ln2_b': ln2_b, 'ffn2_w1': ffn2_w1, 'ffn2_w3': ffn2_w3, 'ffn2_w2': ffn2_w2,
            'ln3_g': ln3_g, 'ln3_b': ln3_b}


def reference(x, c, w_ada, b_ada, ffn1_w1, ffn1_w3, ffn1_w2, ln1_g, ln1_b,
              w_in, conv_w, conv_b, dt_bias, a_log, d_ssd, ssd_norm_w, s5_a_re, s5_a_im,
              s5_log_dt, s5_b_re, s5_b_im, s5_c_re, s5_c_im, s5_d, w_glu, b_glu, w_out,
              ln2_g, ln2_b, ffn2_w1, ffn2_w3, ffn2_w2, ln3_g, ln3_b):
    bsz = x.shape[0]
    cs = jax.nn.silu(c)
    for l in range(DEPTH):
        mod = (cs @ w_ada[l] + b_ada[l]).reshape(bsz, N_MOD, D_MODEL)
        sh1, sc1, g1 = mod[:, 0], mod[:, 1], mod[:, 2]
        sh2, sc2, g2 = mod[:, 3], mod[:, 4], mod[:, 5]
        sh3, sc3, g3 = mod[:, 6], mod[:, 7], mod[:, 8]
        h = modulate(x, sh1, sc1)
        x = layer_norm(ALPHA * x + 0.5 * g1[:, None, :] * swiglu(h, ffn1_w1[l], ffn1_w3[l], ffn1_w2[l]),
                       ln1_g[l], ln1_b[l])
        h = modulate(x, sh2, sc2)
        m = hybrid_mixer(h, w_in[l], conv_w[l], conv_b[l], dt_bias[l], a_log[l], d_ssd[l],
                         ssd_norm_w[l], s5_a_re[l], s5_a_im[l], s5_log_dt[l], s5_b_re[l],
                         s5_b_im[l], s5_c_re[l], s5_c_im[l], s5_d[l], w_glu[l], b_glu[l], w_out[l])
        x = layer_norm(ALPHA * x + g2[:, None, :] * m, ln2_g[l], ln2_b[l])
        h = modulate(x, sh3, sc3)
        x = layer_norm(ALPHA * x + 0.5 * g3[:, None, :] * swiglu(h, ffn2_w1[l], ffn2_w3[l], ffn2_w2[l]),
                       ln3_g[l], ln3_b[l])
    return x
```

```python
import contextlib
import math
import numpy as np
import concourse.bass as bass
import concourse.mybir as mybir
from concourse.bass_utils import run_bass_kernel_spmd

F32 = mybir.dt.float32
BF16 = mybir.dt.bfloat16
AF = mybir.ActivationFunctionType
ALU = mybir.AluOpType

ENGS = ["pe", "act", "dve", "pool", "sp"]
ALPHA = 2.0 ** 0.25
LN_EPS = 1e-5
D_FF = 2816
NF = 22
TWO_PI = 2.0 * math.pi


class Op:
    __slots__ = ("eng", "fn", "deps", "signals", "ticket", "dma_sem", "idx")

    def __init__(self, eng, fn, dma_sem, idx):
        self.eng = eng
        self.fn = fn
        self.deps = []
        self.signals = False
        self.ticket = None
        self.dma_sem = dma_sem
        self.idx = idx


class Sched:
    def __init__(self, same_engine_sync=True):
        self.ops = {e: [] for e in ENGS}
        self.last_w = {}
        self.readers = {}
        self.n = 0
        self.same_engine_sync = same_engine_sync
        self.dma_sems = {}
        self.ndma = 0

    def op(self, eng, fn, reads=(), writes=(), dma_sem=None):
        o = Op(eng, fn, dma_sem, self.n)
        self.n += 1
        deps = {}
        for k in reads:
            w = self.last_w.get(k)
            if w is not None:
                deps[w.idx] = w
        for k in writes:
            w = self.last_w.get(k)
            if w is not None:
                deps[w.idx] = w
            for r in self.readers.get(k, ()):
                deps[r.idx] = r
        for d in deps.values():
            if d is o:
                continue
            if d.dma_sem is None and d.eng == eng:
                if eng == "pe" or not self.same_engine_sync:
                    continue
            o.deps.append(d)
            d.signals = True
        for k in reads:
            self.readers.setdefault(k, []).append(o)
        for k in writes:
            self.last_w[k] = o
            self.readers[k] = []
        self.ops[eng].append(o)
        return o

    def dma(self, eng, out, in_, reads, writes, sem=None):
        if sem is None:
            sem = "u%d" % self.ndma
            self.ndma += 1
        return self.op(eng, lambda e: e.dma_start(out=out, in_=in_), reads, writes, dma_sem=sem)

    def barrier(self):
        keys = list(self.last_w.keys())
        self.op("sp", lambda e: e.nop(), keys, keys + ["__FENCE"])
        for e in ["pe", "act", "dve", "pool"]:
            self.op(e, lambda en: en.nop(), ["__FENCE"], [])

    def emit(self, nc, final_keys=()):
        for e in ENGS:
            c = 0
            for o in self.ops[e]:
                if o.dma_sem is not None:
                    self.dma_sems[o.dma_sem] = self.dma_sems.get(o.dma_sem, 0) + 16
                    o.ticket = self.dma_sems[o.dma_sem]
                elif o.signals:
                    c += 1
                    o.ticket = c
        stack = contextlib.ExitStack()
        sems = {}
        for e in ENGS:
            sems["e_" + e] = stack.enter_context(nc.semaphore("e_" + e))
        for k in self.dma_sems:
            sems["d_" + k] = stack.enter_context(nc.semaphore("d_" + k))
        finals = [self.last_w[k] for k in final_keys if k in self.last_w]

        def semof(o):
            return sems["d_" + o.dma_sem] if o.dma_sem is not None else sems["e_" + o.eng]

        def run(eng_name, eng):
            waited = {}
            for o in self.ops[eng_name]:
                for d in o.deps:
                    s = semof(d)
                    if waited.get(id(s), 0) >= d.ticket:
                        continue
                    eng.wait_ge(s, d.ticket)
                    waited[id(s)] = d.ticket
                ins = o.fn(eng)
                if o.dma_sem is not None:
                    ins.then_inc(semof(o), 16)
                elif o.signals:
                    ins.then_inc(semof(o), 1)
            if eng_name == "sp":
                for d in finals:
                    eng.wait_ge(semof(d), d.ticket)

        with stack:
            with nc.Block() as block:
                @block.tensor
                def _(e):
                    run("pe", e)

                @block.scalar
                def _(e):
                    run("act", e)

                @block.vector
                def _(e):
                    run("dve", e)

                @block.gpsimd
                def _(e):
                    run("pool", e)

                @block.sync
                def _(e):
                    run("sp", e)


WIN_OFFS = [0, 128, 256, 384, 512, 640, 768, 896, 1024, 1152, 1280, 1408,
            1544, 1672, 1800, 1928]
SLOT = 2816
NSLOT = 4

VEC_NAMES = ["ln1g", "ln1b", "ln2g", "ln2b", "ln3g", "ln3b", "convb", "ag1", "ab1", "ag2", "ab2"]
VECB_NAMES = ["a1", "b1", "gt1", "gn2", "bn2", "gt2", "gn3", "bn3", "gt3", "tmp"]


def build(nc, S_LEN, taps=(), limit=None):
    NT = 512
    NTILE = S_LEN // NT
    TOK = 2 * S_LEN
    S = Sched()
    es = contextlib.ExitStack()

    def din(name, shape, dt=F32):
        return nc.dram_tensor(name, list(shape), dt, kind="ExternalInput").ap()

    def sb(name, shape, dt=F32):
        return es.enter_context(nc.sbuf_tensor(name, list(shape), dt))

    x_d = din("x", [TOK, 1024])
    out_d = nc.dram_tensor("out", [TOK, 1024], F32, kind="ExternalOutput").ap()
    cT_d = din("cT", [128, 8, 2])
    wada_d = din("w_ada", [1024, 9216])
    bada_d = din("b_ada", [128, 72])
    ffw1_d = [din("ffn1_w1", [1024, D_FF]), din("ffn2_w1", [1024, D_FF])]
    ffw3_d = [din("ffn1_w3", [1024, D_FF]), din("ffn2_w3", [1024, D_FF])]
    ffw2_d = [din("ffn1_w2", [D_FF, 1024]), din("ffn2_w2", [D_FF, 1024])]
    win_d = din("w_in", [1024, 2056])
    wglu_d = din("w_glu", [512, 512])
    wout_d = din("w_out", [1024, 1024])
    vec_d = din("vecs", [128, 7, 8])
    convw_d = din("convw", [128, 8, 4])
    ssdv_d = din("ssdv", [128, 3, 8])
    ssdc_d = din("ssdc", [128, 3, 4])
    bglu_d = din("bglu", [128, 4])
    s5v_d = din("s5v", [128, 3, 16])
    s5bt_d = din("s5bt", [128, 2, 16, 128])
    s5ct_d = din("s5ct", [128, 2, 16, 128])
    cst_d = din("cst", [128, 3, 128])

    w1s_d = nc.dram_tensor("w1s", [2, NF, 128, 2048], BF16).ap()
    w2s_d = nc.dram_tensor("w2s", [2, 8, 128, SLOT], BF16).ap()
    wins_d = nc.dram_tensor("wins", [16, 128, 1024], BF16).ap()
    wglus_d = nc.dram_tensor("wglus", [4, 128, 512], BF16).ap()
    wouts_d = nc.dram_tensor("wouts", [8, 128, 1024], BF16).ap()

    tap_d = {}
    for nm, shp in taps:
        tap_d[nm] = nc.dram_tensor("tap_" + nm, list(shp), F32, kind="ExternalOutput").ap()

    R = sb("R", [128, 8, NT])
    XIN = sb("XIN", [128, 4, 1024])
    H = sb("H", [128, 8, NT], BF16)
    G = sb("G", [128, NF, NT], BF16)
    WR = sb("WR", [128, NSLOT, SLOT], BF16)
    SG = sb("SG", [128, 2, NT], BF16)
    MEAN = sb("MEAN", [128, NT])
    RSTD = sb("RSTD", [128, NT])
    VEC = sb("VEC", [128, len(VEC_NAMES), 8])
    VECB = sb("VECB", [128, len(VECB_NAMES), 2, 8])
    MOD = sb("MOD", [128, 72, 2])
    CST = sb("CST", [128, 3, 128])
    IDB = sb("IDB", [128, 128], BF16)
    ONELN = sb("ONELN", [128, 128], BF16)
    ONERMS = sb("ONERMS", [128, 128], BF16)
    ONEF = sb("ONEF", [128, 128])
    CONVW = sb("CONVW", [128, 8, 4])
    SSDV = sb("SSDV", [128, 3, 8])
    SSDC = sb("SSDC", [128, 3, 4])
    BGLU = sb("BGLU", [128, 4])
    WDT = sb("WDT", [128, 8, 8], BF16)
    SZ = sb("SZ", [128, 4, NT], BF16)
    XBC = sb("XBC", [128, 8, NT + 3], BF16)
    XS = sb("XS", [128, 4, NT], BF16)
    BC = sb("BC", [128, 4, NT], BF16)
    U = sb("U", [128, 4, NT], BF16)
    CT32 = sb("CT32", [128, NT])
    DT = sb("DT", [128, 4, 8])
    ADT = sb("ADT", [128, 4, 8])
    ACS = sb("ACS", [128, 8])
    DEC = sb("DEC", [128, 8])
    RHSA = sb("RHSA", [128, 8, 128])
    DIFF = sb("DIFF", [128, 8, 128])
    LMAT = sb("LMAT", [128, 8, 128], BF16)
    EROW = sb("EROW", [128, 8, 128])
    CBM = sb("CBM", [128, 2, 128], BF16)
    MT = sb("MT", [128, 8, 128], BF16)
    CSC = sb("CSC", [128, 8, 128], BF16)
    XDT = sb("XDT", [128, 8, 64], BF16)
    XDD = sb("XDD", [128, 8, 64], BF16)
    BTOK = sb("BTOK", [128, 2, 128], BF16)
    HT = sb("HT", [128, 8, 64])
    HTB = sb("HTB", [128, 8, 64], BF16)
    YS = sb("YS", [128, 4, NT])
    YCAT = sb("YCAT", [128, 8, NT], BF16)
    S5V = sb("S5V", [128, 12, 16])
    COST = sb("COST", [128, 16, 128])
    SINT = sb("SINT", [128, 16, 128])
    BBR = sb("BBR", [128, 16, 128], BF16)
    BBI = sb("BBI", [128, 16, 128], BF16)
    CRE = sb("CRE", [128, 16, 128], BF16)
    CIMN = sb("CIMN", [128, 16, 128], BF16)
    BPR = sb("BPR", [128, 8, 128])
    BPI = sb("BPI", [128, 8, 128])
    TMPA = RHSA
    TMPB = DIFF
    XRB = LMAT
    XIB = MT
    WRE = XIN[:, 0, :].rearrange("p (j t) -> p j t", t=128)
    WIM = XIN[:, 1, :].rearrange("p (j t) -> p j t", t=128)
    XRF = XIN[:, 2, :].rearrange("p (j t) -> p j t", t=128)
    XIF = XIN[:, 3, :].rearrange("p (j t) -> p j t", t=128)
    XST = sb("XST", [128, 2, 16])
    PS = es.enter_context(nc.psum_tensor("PS", [128, 8, 512], F32))

    IDF = CST[:, 0, :]
    TRI = CST[:, 1, :]
    IOTA = CST[:, 2, :]

    def V(name, kt):
        return VEC[:, VEC_NAMES.index(name), kt:kt + 1]

    def VB(name, b, kt):
        return VECB[:, VECB_NAMES.index(name), b, kt:kt + 1]

    Rk = ["R%d" % k for k in range(8)]
    Hk = ["H%d" % k for k in range(8)]
    Gk = ["G%d" % f for f in range(NF)]

    def psb(i):
        return PS[:, i, :]

    def psbf(i):
        return PS[:, i, :].bitcast(BF16)

    rr = {"cast": 0, "slot": 0}

    def cast_op(out, in_, reads, writes):
        e = ["act", "dve"][rr["cast"] % 2]
        rr["cast"] += 1
        if e == "act":
            S.op("act", lambda en: en.copy(out, in_), reads, writes)
        else:
            S.op(e, lambda en: en.tensor_copy(out, in_), reads, writes)

    S.dma("sp", CST[:], cst_d, [], ["CST"])
    S.dma("sp", VEC[:, 0:7, :], vec_d, [], ["VEC"])
    S.dma("sp", CONVW[:], convw_d, [], ["CONVW"])
    S.dma("sp", SSDV[:], ssdv_d, [], ["SSDV"])
    S.dma("sp", SSDC[:], ssdc_d, [], ["SSDC"])
    S.dma("sp", BGLU[:], bglu_d, [], ["BGLU"])
    S.dma("sp", S5V[:, 0:3, :], s5v_d, [], ["S5V"])
    S.dma("sp", MOD[:].rearrange("p m b -> p (m b)")[:, 0:16], cT_d.rearrange("p k b -> p (k b)"), [], ["MOD"])
    S.op("act", lambda e: e.copy(IDB[:], IDF), ["CST"], ["IDB"])
    S.op("pool", lambda e: e.memset(ONELN[:], 1.0 / 1024.0), [], ["ONELN"])
    S.op("pool", lambda e: e.memset(ONERMS[:], 1.0 / 256.0), [], ["ONERMS"])
    S.op("pool", lambda e: e.memset(ONEF[:], 1.0), [], ["ONEF"])

    CS = TMPB[:, 0, 0:16].rearrange("p (k b) -> p k b", b=2)
    S.op("act", lambda e: e.activation(out=TMPB[:, 0, 0:16], in_=MOD[:].rearrange("p m b -> p (m b)")[:, 0:16],
                                       func=AF.Silu), ["MOD"], ["DIFF"])
    BADA = TMPA[:, 0, 0:72]
    S.dma("sp", BADA, bada_d, [], ["RHSA"])
    slab_keys = [["G%d" % f for f in range(0, 11)], ["G%d" % f for f in range(11, 22)]]
    Gf32 = G[:].rearrange("p f t -> p (f t)").bitcast(F32)
    si = 0
    for q in range(4):
        for kt in range(8):
            buf = si % 2
            slab = Gf32[:, buf * 2816: buf * 2816 + 2304]
            S.dma("sp", slab, wada_d[kt * 128:(kt + 1) * 128, q * 2304:(q + 1) * 2304], [], slab_keys[buf],
                  sem="slab%d" % buf)

            pbank = si % 2

            def mm(e, slab=slab, kt=kt, pbank=pbank):
                ins = None
                for m in range(18):
                    ins = e.matmul(PS[:, pbank, m * 2:m * 2 + 2], lhsT=slab[:, m * 128:(m + 1) * 128],
                                   rhs=CS[:, kt, :], start=True, stop=True)
                return ins
            S.op("pe", mm, slab_keys[buf] + ["DIFF"], ["PS%d" % pbank])
            dst = MOD[:, q * 18:(q + 1) * 18, :]
            src = PS[:, pbank, 0:36].rearrange("p (m b) -> p m b", b=2)
            if kt == 0:
                S.op("dve", lambda e, dst=dst, src=src: e.tensor_copy(dst, src), ["PS%d" % pbank], ["MOD"])
            else:
                S.op("dve", lambda e, dst=dst, src=src: e.tensor_tensor(out=dst, in0=dst, in1=src, op=ALU.add),
                     ["PS%d" % pbank, "MOD"], ["MOD"])
            si += 1
    S.op("dve", lambda e: e.tensor_tensor(out=MOD[:], in0=MOD[:],
                                          in1=BADA.unsqueeze(2).to_broadcast([128, 72, 2]), op=ALU.add),
         ["MOD", "RHSA"], ["MOD"])

    def modv(j, b):
        return MOD[:, j * 8:(j + 1) * 8, b]

    def vb(name, b):
        return VECB[:, VECB_NAMES.index(name), b, :]

    def vv(name):
        return VEC[:, VEC_NAMES.index(name), :]

    def dve(fn, reads, writes):
        S.op("dve", fn, reads, writes)

    for b in range(2):
        dve(lambda e, b=b: e.tensor_scalar_add(vb("a1", b), modv(1, b), 1.0), ["MOD"], ["VECB"])
        dve(lambda e, b=b: e.tensor_copy(vb("b1", b), modv(0, b)), ["MOD"], ["VECB"])
        dve(lambda e, b=b: e.tensor_scalar_mul(vb("gt1", b), modv(2, b), 0.5), ["MOD"], ["VECB"])
        dve(lambda e, b=b: e.tensor_copy(vb("gt2", b), modv(5, b)), ["MOD"], ["VECB"])
        dve(lambda e, b=b: e.tensor_scalar_mul(vb("gt3", b), modv(8, b), 0.5), ["MOD"], ["VECB"])
        for (gn, bn, lg, lb, jsc, jsh) in [("gn2", "bn2", "ln1g", "ln1b", 4, 3), ("gn3", "bn3", "ln2g", "ln2b", 7, 6)]:
            dve(lambda e, b=b, jsc=jsc: e.tensor_scalar_add(vb("tmp", b), modv(jsc, b), 1.0), ["MOD", "VECB"], ["VECB"])
            dve(lambda e, b=b, gn=gn, lg=lg: e.tensor_tensor(out=vb(gn, b), in0=vv(lg), in1=vb("tmp", b), op=ALU.mult),
                ["VEC", "VECB"], ["VECB"])
            dve(lambda e, b=b, bn=bn, lb=lb: e.tensor_tensor(out=vb(bn, b), in0=vv(lb), in1=vb("tmp", b), op=ALU.mult),
                ["VEC", "VECB"], ["VECB"])
            dve(lambda e, b=b, bn=bn, jsh=jsh: e.tensor_tensor(out=vb(bn, b), in0=vb(bn, b), in1=modv(jsh, b), op=ALU.add),
                ["MOD", "VECB"], ["VECB"])
    for (ag, ab, lg, lb) in [("ag1", "ab1", "ln1g", "ln1b"), ("ag2", "ab2", "ln2g", "ln2b")]:
        dve(lambda e, ag=ag, lg=lg: e.tensor_scalar_mul(vv(ag), vv(lg), ALPHA), ["VEC"], ["VEC"])
        dve(lambda e, ab=ab, lb=lb: e.tensor_scalar_mul(vv(ab), vv(lb), ALPHA), ["VEC"], ["VEC"])

    S.op("act", lambda e: e.activation(out=SSDV[:, 2, :], in_=SSDV[:, 1, :], func=AF.Exp), ["SSDV"], ["SSDV"])
    dve(lambda e: e.tensor_scalar_mul(SSDV[:, 2, :], SSDV[:, 2, :], -1.0), ["SSDV"], ["SSDV"])
    DTB = SSDV[:, 0, :]
    ANEG = SSDV[:, 2, :]

    I32 = mybir.dt.int32
    HPI = sb("HPI", [128, 1])
    S.op("pool", lambda e: e.memset(HPI[:], 0.5 * math.pi), [], ["HPI"])
    Rflat = R[:].rearrange("p k t -> p (k t)")
    YSflat = YS[:].rearrange("p k t -> p (k t)")
    XINflat = XIN[:].rearrange("p j d -> p (j d)")

    def sincos(ang, osin, ocos, n, kin, ksin, kcos):
        KI = Rflat[:, 2048:2048 + n].bitcast(I32)
        RR = Rflat[:, 2048:2048 + n]
        S_ = YSflat[:, 0:n]
        C_ = XINflat[:, 0:n]
        dve(lambda e: e.tensor_scalar_mul(KI, ang, 1.0 / TWO_PI), kin, Rk)
        dve(lambda e: e.tensor_copy(S_, KI), Rk, ["YS"])
        dve(lambda e: e.scalar_tensor_tensor(out=RR, in0=S_, scalar=-TWO_PI, in1=ang, op0=ALU.mult, op1=ALU.add),
            ["YS"] + kin + Rk, Rk)
        S.op("act", lambda e: e.activation(out=S_, in_=RR, func=AF.Sin, scale=0.25), Rk, ["YS"])
        S.op("act", lambda e: e.activation(out=C_, in_=RR, func=AF.Sin, scale=0.25, bias=HPI[:]), Rk + ["HPI"], ["XIN"])
        dve(lambda e: e.tensor_tensor(out=C_, in0=C_, in1=S_, op=ALU.mult), ["XIN", "YS"], ["XIN"])
        dve(lambda e: e.tensor_scalar_mul(C_, C_, 2.0), ["XIN"], ["XIN"])
        dve(lambda e: e.tensor_tensor(out=S_, in0=S_, in1=S_, op=ALU.mult), ["YS"], ["YS"])
        dve(lambda e: e.tensor_scalar(out=S_, in0=S_, scalar1=-2.0, scalar2=1.0, op0=ALU.mult, op1=ALU.add),
            ["YS"], ["YS"])
        dve(lambda e: e.tensor_tensor(out=S_, in0=S_, in1=C_, op=ALU.mult), ["XIN", "YS"], ["YS"])
        dve(lambda e: e.tensor_scalar_mul(osin, S_, 2.0), ["YS"], ksin)
        dve(lambda e: e.tensor_tensor(out=C_, in0=C_, in1=C_, op=ALU.mult), ["XIN"], ["XIN"])
        dve(lambda e: e.tensor_scalar(out=ocos, in0=C_, scalar1=-2.0, scalar2=1.0, op0=ALU.mult, op1=ALU.add),
            ["XIN"], kcos)

    def s5(i):
        return S5V[:, i, :]
    AR, AI, LDT, SDT, TH, MAG, CO, SI, FRE, FIM, T0, T1 = [s5(i) for i in range(12)]
    K5 = ["S5V"]
    S.op("act", lambda e: e.activation(out=SDT, in_=LDT, func=AF.Exp), K5, K5)
    dve(lambda e: e.tensor_tensor(out=TH, in0=SDT, in1=AI, op=ALU.mult), K5, K5)
    dve(lambda e: e.tensor_tensor(out=T0, in0=SDT, in1=AR, op=ALU.mult), K5, K5)
    S.op("act", lambda e: e.activation(out=MAG, in_=T0, func=AF.Exp), K5, K5)
    sincos(TH, SI, CO, 16, K5, K5, K5)
    dve(lambda e: e.tensor_tensor(out=CO, in0=CO, in1=MAG, op=ALU.mult), K5, K5)
    dve(lambda e: e.tensor_tensor(out=SI, in0=SI, in1=MAG, op=ALU.mult), K5, K5)
    dve(lambda e: e.tensor_scalar_add(CO, CO, -1.0), K5, K5)
    dve(lambda e: e.tensor_tensor(out=T0, in0=AR, in1=AR, op=ALU.mult), K5, K5)
    dve(lambda e: e.tensor_tensor(out=T1, in0=AI, in1=AI, op=ALU.mult), K5, K5)
    dve(lambda e: e.tensor_tensor(out=T0, in0=T0, in1=T1, op=ALU.add), K5, K5)
    dve(lambda e: e.reciprocal(T0, T0), K5, K5)
    dve(lambda e: e.tensor_tensor(out=FRE, in0=CO, in1=AR, op=ALU.mult), K5, K5)
    dve(lambda e: e.tensor_tensor(out=T1, in0=SI, in1=AI, op=ALU.mult), K5, K5)
    dve(lambda e: e.tensor_tensor(out=FRE, in0=FRE, in1=T1, op=ALU.add), K5, K5)
    dve(lambda e: e.tensor_tensor(out=FRE, in0=FRE, in1=T0, op=ALU.mult), K5, K5)
    dve(lambda e: e.tensor_tensor(out=FIM, in0=SI, in1=AR, op=ALU.mult), K5, K5)
    dve(lambda e: e.tensor_tensor(out=T1, in0=CO, in1=AI, op=ALU.mult), K5, K5)
    dve(lambda e: e.tensor_tensor(out=FIM, in0=FIM, in1=T1, op=ALU.subtract), K5, K5)
    dve(lambda e: e.tensor_tensor(out=FIM, in0=FIM, in1=T0, op=ALU.mult), K5, K5)

    ANG = R[:].rearrange("p k t -> p (k t)")[:, 0:2048].rearrange("p (g t) -> p g t", t=128)
    iota_b = IOTA.unsqueeze(1).to_broadcast([128, 16, 128])
    th_b = TH.unsqueeze(2).to_broadcast([128, 16, 128])
    dve(lambda e: e.tensor_tensor(out=ANG, in0=iota_b, in1=th_b, op=ALU.mult), K5 + ["CST"], Rk)
    sincos(ANG.rearrange("p g t -> p (g t)"), SINT[:].rearrange("p g t -> p (g t)"),
           COST[:].rearrange("p g t -> p (g t)"), 2048, Rk, ["SINT"], ["COST"])

    BT = XIN[:].rearrange("p j d -> p (j d)").rearrange("p (c g k) -> p c g k", c=2, g=16)
    S.dma("sp", BT, s5bt_d, [], ["XIN"])
    fre_b = FRE.unsqueeze(2).to_broadcast([128, 16, 128])
    fim_b = FIM.unsqueeze(2).to_broadcast([128, 16, 128])
    BbR = R[:].rearrange("p k t -> p (k t)")[:, 0:2048].rearrange("p (g t) -> p g t", t=128)
    BbI = R[:].rearrange("p k t -> p (k t)")[:, 2048:4096].rearrange("p (g t) -> p g t", t=128)
    T5 = YS[:].rearrange("p k t -> p (k t)").rearrange("p (g t) -> p g t", t=128)
    RK_XIN = Rk + ["XIN"]
    dve(lambda e: e.tensor_tensor(out=BbR, in0=BT[:, 0], in1=fre_b, op=ALU.mult), K5 + ["XIN"] + ["SINT", "COST"], Rk)
    dve(lambda e: e.tensor_tensor(out=T5, in0=BT[:, 1], in1=fim_b, op=ALU.mult), K5 + ["XIN"], ["YS"])
    dve(lambda e: e.tensor_tensor(out=BbR, in0=BbR, in1=T5, op=ALU.subtract), Rk + ["YS"], Rk)
    dve(lambda e: e.tensor_tensor(out=BbI, in0=BT[:, 1], in1=fre_b, op=ALU.mult), K5 + ["XIN"], Rk)
    dve(lambda e: e.tensor_tensor(out=T5, in0=BT[:, 0], in1=fim_b, op=ALU.mult), K5 + ["XIN"] + Rk, ["YS"])
    dve(lambda e: e.tensor_tensor(out=BbI, in0=BbI, in1=T5, op=ALU.add), Rk + ["YS"], Rk)
    for (src, dst, key) in [(BbR, BBR, "BBR"), (BbI, BBI, "BBI")]:
        for q in range(4):
            def tp(e, src=src, q=q):
                ins = None
                for j in range(4):
                    ins = e.transpose(PS[:, 1 + q % 2, j * 128:(j + 1) * 128], src[:, q * 4 + j, :], IDF)
                return ins
            S.op("pe", tp, Rk + ["CST"], ["PS%d" % (1 + q % 2)])
            S.op("act", lambda e, dst=dst, q=q: e.copy(dst[:, q * 4:(q + 1) * 4, :].rearrange("p g t -> p (g t)"),
                                                      PS[:, 1 + q % 2, :]), ["PS%d" % (1 + q % 2)], [key])
    CTt = XIN[:].rearrange("p j d -> p (j d)").rearrange("p (c g k) -> p c g k", c=2, g=16)
    S.dma("sp", CTt, s5ct_d, [], ["XIN"])
    S.op("act", lambda e: e.copy(CRE[:], CTt[:, 0]), ["XIN"], ["CRE"])
    S.op("act", lambda e: e.mul(CIMN[:], CTt[:, 1], -1.0), ["XIN"], ["CIMN"])

    cvi = [Gf32[:, 0:2816], Gf32[:, 2816:5632]]
    cvk = slab_keys
    cvn = [0]

    def convert(srcs, width, dst):
        i = cvn[0] % 2
        cvn[0] += 1
        slot = i
        off = 0
        for (ap, w3) in srcs:
            n = w3[0] * w3[1]
            S.dma("sp", cvi[i][:, off:off + n].rearrange("p (a c) -> p a c", c=w3[1]), ap, [], cvk[i],
                  sem="cvi%d_%d" % (i, off))
            off += n
        cast_op(WR[:, slot, 0:width], cvi[i][:, 0:width], cvk[i], ["WR%d" % slot])
        if dst is not None:
            S.dma("act", dst, WR[:, slot, 0:width], ["WR%d" % slot], ["SCR"], sem="cvo%d" % i)

    def colchunk(w, c0, nk):
        return (w[:, c0:c0 + 128].rearrange("(k p) c -> p k c", p=128), (nk, 128))

    for ff in range(2):
        for f in range(NF):
            convert([colchunk(ffw1_d[ff], f * 128, 8), colchunk(ffw3_d[ff], f * 128, 8)], 2048, w1s_d[ff, f])
        for m in range(8):
            convert([colchunk(ffw2_d[ff], m * 128, NF)], SLOT, w2s_d[ff, m])
    for mt in range(16):
        convert([colchunk(win_d, WIN_OFFS[mt], 8)], 1024, wins_d[mt])
    for mt in range(4):
        convert([colchunk(wglu_d, mt * 128, 4)], 512, wglus_d[mt])
    for m in range(8):
        convert([colchunk(wout_d, m * 128, 8)], 1024, wouts_d[m])
    i = cvn[0] % 2
    S.dma("sp", cvi[i][:, 0:64].rearrange("p (a c) -> p a c", c=8),
          win_d[:, 1536:1544].rearrange("(k p) c -> p k c", p=128), [], cvk[i])
    S.op("dve", lambda e: e.tensor_copy(WDT[:].rearrange("p k c -> p (k c)"), cvi[i][:, 0:64]), cvk[i], ["WDT"])

    S.barrier()

    def wload(src, width):
        slot = rr["slot"] % NSLOT
        rr["slot"] += 1
        S.dma("sp", WR[:, slot, 0:width], src, ["SCR"], ["WR%d" % slot], sem="wr%d" % slot)
        return slot

    def load_x(row0):
        S.dma("sp", XIN[:], x_d[row0:row0 + NT, :].rearrange("(j p) d -> p j d", p=128), [], ["XIN"], sem="xin")

    def tap(name, idx, ap, keys):
        if name in tap_d:
            S.dma("sp", tap_d[name][idx], ap, keys, ["tap_" + name])

    def layer_norm(b, ag, ab, gn, bn, final=False):
        SQ = G
        for kt in range(8):
            S.op("act", lambda e, kt=kt: e.copy(H[:, kt, :], R[:, kt, :]), [Rk[kt]], [Hk[kt]])
            S.op("act", lambda e, kt=kt: e.activation(out=SQ[:, kt, :], in_=R[:, kt, :], func=AF.Square),
                 [Rk[kt]], [Gk[kt]])

        def st(e):
            ins = None
            for kt in range(8):
                e.matmul(PS[:, 6, :], lhsT=ONELN[:], rhs=H[:, kt, :], start=(kt == 0), stop=(kt == 7))
            for kt in range(8):
                ins = e.matmul(PS[:, 7, :], lhsT=ONELN[:], rhs=SQ[:, kt, :], start=(kt == 0), stop=(kt == 7))
            return ins
        S.op("pe", st, Hk + Gk[0:8] + ["ONELN"], ["PS6", "PS7"])
        S.op("act", lambda e: e.copy(MEAN[:], PS[:, 6, :]), ["PS6"], ["MEAN"])
        S.op("dve", lambda e: e.tensor_tensor(out=RSTD[:], in0=MEAN[:], in1=MEAN[:], op=ALU.mult), ["MEAN"], ["RSTD"])
        S.op("dve", lambda e: e.tensor_tensor(out=RSTD[:], in0=PS[:, 7, :], in1=RSTD[:], op=ALU.subtract),
             ["PS7", "RSTD"], ["RSTD"])
        S.op("dve", lambda e: e.tensor_scalar_add(RSTD[:], RSTD[:], LN_EPS), ["RSTD"], ["RSTD"])
        S.op("act", lambda e: e.activation(out=RSTD[:], in_=RSTD[:], func=AF.Sqrt), ["RSTD"], ["RSTD"])
        S.op("dve", lambda e: e.reciprocal(RSTD[:], RSTD[:]), ["RSTD"], ["RSTD"])
        for kt in range(8):
            eng = "dve"
            S.op(eng, lambda e, kt=kt: e.tensor_tensor(out=R[:, kt, :], in0=R[:, kt, :], in1=MEAN[:], op=ALU.subtract),
                 [Rk[kt], "MEAN"], [Rk[kt]])
            S.op(eng, lambda e, kt=kt: e.tensor_tensor(out=R[:, kt, :], in0=R[:, kt, :], in1=RSTD[:], op=ALU.mult),
                 [Rk[kt], "RSTD"], [Rk[kt]])
            if not final:
                S.op("act", lambda e, kt=kt: e.activation(out=H[:, kt, :], in_=R[:, kt, :], func=AF.Identity,
                                                          bias=VB(bn, b, kt), scale=VB(gn, b, kt)),
                     [Rk[kt], "VECB"], [Hk[kt]])
            S.op("act", lambda e, kt=kt: e.activation(out=R[:, kt, :], in_=R[:, kt, :], func=AF.Identity,
                                                      bias=V(ab, kt), scale=V(ag, kt)),
                 [Rk[kt], "VEC"], [Rk[kt]])

    def ffn(ff, b, gate):
        pair = 0
        for f in range(NF):
            slot = wload(w1s_d[ff, f], 2048)
            pa, pb = (0, 1) if pair == 0 else (2, 3)
            pair ^= 1

            def mm(e, slot=slot, pa=pa, pb=pb):
                ins = None
                for kt in range(8):
                    e.matmul(PS[:, pa, :], lhsT=WR[:, slot, kt * 128:(kt + 1) * 128], rhs=H[:, kt, :],
                             start=(kt == 0), stop=(kt == 7))
                for kt in range(8):
                    ins = e.matmul(PS[:, pb, :], lhsT=WR[:, slot, 1024 + kt * 128:1024 + (kt + 1) * 128],
                                   rhs=H[:, kt, :], start=(kt == 0), stop=(kt == 7))
                return ins
            S.op("pe", mm, ["WR%d" % slot] + Hk, ["PS%d" % pa, "PS%d" % pb])
            sg = f % 2
            S.op("act", lambda e, pa=pa, sg=sg: e.activation(out=SG[:, sg, :], in_=PS[:, pa, :], func=AF.Silu),
                 ["PS%d" % pa], ["SG%d" % sg])
            S.op("dve", lambda e, pb=pb, sg=sg, f=f: e.tensor_tensor(out=G[:, f, :], in0=PS[:, pb, :], in1=SG[:, sg, :],
                                                                   op=ALU.mult),
                 ["PS%d" % pb, "SG%d" % sg], [Gk[f]])
        for m in range(8):
            slot = wload(w2s_d[ff, m], SLOT)
            pb_ = 4 + m % 2

            def mm2(e, slot=slot, pb_=pb_):
                ins = None
                for f in range(NF):
                    ins = e.matmul(PS[:, pb_, :], lhsT=WR[:, slot, f * 128:(f + 1) * 128], rhs=G[:, f, :],
                                   start=(f == 0), stop=(f == NF - 1))
                return ins
            S.op("pe", mm2, ["WR%d" % slot] + Gk, ["PS%d" % pb_])
            S.op("dve", lambda e, m=m, pb_=pb_: e.scalar_tensor_tensor(out=R[:, m, :], in0=PS[:, pb_, :],
                                                                     scalar=VB(gate, b, m), in1=R[:, m, :],
                                                                     op0=ALU.mult, op1=ALU.add),
                 ["PS%d" % pb_, Rk[m], "VECB"], [Rk[m]])

    def mixer(b, first):
        WREk = ["WRE%d" % j for j in range(8)]
        WIMk = ["WIM%d" % j for j in range(8)]
        for mt in range(16):
            slot = wload(wins_d[mt], 1024)
            pbk = mt % 4

            def mm(e, slot=slot, pbk=pbk):
                ins = None
                for kt in range(8):
                    ins = e.matmul(PS[:, pbk, :], lhsT=WR[:, slot, kt * 128:(kt + 1) * 128], rhs=H[:, kt, :],
                                   start=(kt == 0), stop=(kt == 7))
                return ins
            S.op("pe", mm, ["WR%d" % slot] + Hk, ["PS%d" % pbk])
            if mt < 4:
                S.op("act", lambda e, mt=mt, pbk=pbk: e.activation(out=SZ[:, mt, :], in_=PS[:, pbk, :], func=AF.Silu),
                     ["PS%d" % pbk], ["SZ%d" % mt])
            elif mt < 12:
                ct = mt - 4
                if first:
                    S.op("dve", lambda e, ct=ct: e.memset(XBC[:, ct, 0:3], 0.0), [], ["XBC%d" % ct])
                else:
                    S.op("dve", lambda e, ct=ct: e.tensor_copy(XBC[:, ct, 0:3], XBC[:, ct, NT:NT + 3]),
                         ["XBC%d" % ct], ["XBC%d" % ct])
                S.op("act", lambda e, ct=ct, pbk=pbk: e.copy(XBC[:, ct, 3:NT + 3], PS[:, pbk, :]),
                     ["PS%d" % pbk, "XBC%d" % ct], ["XBC%d" % ct])
            else:
                S.op("dve", lambda e, mt=mt, pbk=pbk: e.tensor_copy(U[:, mt - 12, :], PS[:, pbk, :]),
                     ["PS%d" % pbk], ["U%d" % (mt - 12)])

        def mmdt(e):
            ins = None
            for c in range(4):
                for kt in range(8):
                    ins = e.matmul(PS[:, 4, c * 8:(c + 1) * 8], lhsT=H[:, kt, c * 128:(c + 1) * 128], rhs=WDT[:, kt, :],
                                   start=(kt == 0), stop=(kt == 7))
            return ins
        S.op("pe", mmdt, Hk + ["WDT"], ["PS4"])
        S.op("dve", lambda e: e.tensor_tensor(out=DT[:], in0=PS[:, 4, 0:32].rearrange("p (c h) -> p c h", h=8),
                                              in1=DTB.unsqueeze(1).to_broadcast([128, 4, 8]), op=ALU.add),
             ["PS4", "SSDV"], ["DT"])
        S.op("act", lambda e: e.activation(out=DT[:], in_=DT[:], func=AF.Exp), ["DT"], ["DT"])
        S.op("act", lambda e: e.activation(out=DT[:], in_=DT[:], func=AF.Ln, bias=1.0), ["DT"], ["DT"])
        S.op("dve", lambda e: e.tensor_tensor(out=ADT[:], in0=DT[:], in1=ANEG.unsqueeze(1).to_broadcast([128, 4, 8]),
                                              op=ALU.mult), ["DT", "SSDV"], ["ADT"])
        for ct in range(8):
            eng = "dve"
            kx = "XBC%d" % ct
            S.op(eng, lambda e, ct=ct: e.tensor_scalar_mul(CT32[:], XBC[:, ct, 0:NT], CONVW[:, ct, 0:1]),
                 [kx, "CONVW"], ["CT32"])
            for k in range(1, 4):
                S.op(eng, lambda e, ct=ct, k=k: e.scalar_tensor_tensor(out=CT32[:], in0=XBC[:, ct, k:k + NT],
                                                                      scalar=CONVW[:, ct, k:k + 1], in1=CT32[:],
                                                                      op0=ALU.mult, op1=ALU.add),
                     [kx, "CONVW", "CT32"], ["CT32"])
            if ct < 4:
                dst, dk = XS[:, ct, :], "XS%d" % ct
            else:
                dst, dk = BC[:, ct - 4, :], "BC%d" % (ct - 4)
            S.op("act", lambda e, dst=dst, ct=ct: e.activation(out=dst, in_=CT32[:], func=AF.Silu, bias=V("convb", ct)),
                 ["CT32", "VEC"], [dk])
        if first:
            S.op("dve", lambda e: e.memset(HT[:], 0.0), [], ["HT"])
            S.op("dve", lambda e: e.memset(HTB[:], 0.0), [], ["HTB"])
            S.op("dve", lambda e: e.memset(XST[:], 0.0), [], ["XST"])
        XSk = ["XS%d" % i for i in range(4)]
        for c in range(4):
            cs = slice(c * 128, (c + 1) * 128)
            S.op("dve", lambda e, c=c: e.tensor_tensor(out=RHSA[:], in0=TRI.unsqueeze(1).to_broadcast([128, 8, 128]),
                                                      in1=ADT[:, c, :].unsqueeze(2).to_broadcast([128, 8, 128]),
                                                      op=ALU.mult), ["ADT", "CST"], ["RHSA"])

            def mmcs(e, c=c):
                e.matmul(PS[:, 2, 256:264], lhsT=TRI, rhs=ADT[:, c, :], start=True, stop=True)
                e.matmul(PS[:, 0, :], lhsT=ONEF[:], rhs=RHSA[:, 0:4, :].rearrange("p h l -> p (h l)"), start=True, stop=True)
                return e.matmul(PS[:, 1, :], lhsT=ONEF[:], rhs=RHSA[:, 4:8, :].rearrange("p h l -> p (h l)"),
                                start=True, stop=True)
            S.op("pe", mmcs, ["ADT", "CST", "RHSA", "ONEF"], ["PS0", "PS1", "PS2"])
            AROW = PS[:, 0:2, :].rearrange("p a (h l) -> p (a h) l", l=128)
            S.op("act", lambda e: e.copy(ACS[:], PS[:, 2, 256:264]), ["PS2"], ["ACS"])
            S.op("dve", lambda e, AROW=AROW: e.tensor_tensor(out=DIFF[:], in0=AROW,
                                                            in1=ACS[:].unsqueeze(2).to_broadcast([128, 8, 128]),
                                                            op=ALU.subtract), ["PS0", "PS1", "ACS"], ["DIFF"])
            S.op("dve", lambda e: e.tensor_scalar_min(DIFF[:], DIFF[:], 0.0), ["DIFF"], ["DIFF"])
            S.op("act", lambda e: e.activation(out=LMAT[:], in_=DIFF[:], func=AF.Exp), ["DIFF"], ["LMAT"])
            S.op("act", lambda e, AROW=AROW: e.activation(out=EROW[:], in_=AROW, func=AF.Exp), ["PS0", "PS1"], ["EROW"])
            S.op("dve", lambda e, AROW=AROW: e.tensor_tensor(out=DEC[:], in0=AROW[:, :, 127], in1=ACS[:], op=ALU.subtract),
                 ["PS0", "PS1", "ACS"], ["DEC"])
            S.op("act", lambda e: e.activation(out=DEC[:], in_=DEC[:], func=AF.Exp), ["DEC"], ["DEC"])

            def mmcb(e, cs=cs):
                e.matmul(PS[:, 2, 0:128], lhsT=BC[:, 0, cs], rhs=BC[:, 2, cs], start=True, stop=True)
                return e.matmul(PS[:, 2, 128:256], lhsT=BC[:, 1, cs], rhs=BC[:, 3, cs], start=True, stop=True)
            S.op("pe", mmcb, ["BC0", "BC1", "BC2", "BC3"], ["PS2"])
            S.op("dve", lambda e: e.tensor_tensor(out=CBM[:], in0=PS[:, 2, 0:256].rearrange("p (g l) -> p g l", l=128),
                                                  in1=TRI.unsqueeze(1).to_broadcast([128, 2, 128]), op=ALU.mult),
                 ["PS2", "CST"], ["CBM"])
            for g in range(2):
                hs = slice(4 * g, 4 * g + 4)
                S.op("dve", lambda e, g=g, hs=hs: e.tensor_tensor(
                    out=MT[:, hs, :], in0=LMAT[:, hs, :],
                    in1=CBM[:, g:g + 1, :].to_broadcast([128, 4, 128]), op=ALU.mult), ["LMAT", "CBM"], ["MT"])
                S.op("dve", lambda e, g=g, hs=hs, cs=cs: e.tensor_tensor(
                    out=CSC[:, hs, :], in0=EROW[:, hs, :],
                    in1=BC[:, 2 + g:3 + g, cs].to_broadcast([128, 4, 128]), op=ALU.mult),
                    ["EROW", "BC2", "BC3"], ["CSC"])
            PT = psbf(3)

            def mmtr(e, cs=cs, PT=PT):
                ins = None
                for i in range(4):
                    e.transpose(PT[:, i * 128:(i + 1) * 128], XS[:, i, cs], IDB[:])
                for g in range(2):
                    ins = e.transpose(PT[:, 512 + g * 128:512 + (g + 1) * 128], BC[:, g, cs], IDB[:])
                return ins
            S.op("pe", mmtr, XSk + ["BC0", "BC1", "IDB"], ["PS3"])
            S.op("dve", lambda e, c=c, PT=PT: e.tensor_tensor(
                out=XDT[:], in0=PT[:, 0:512].rearrange("p (h q) -> p h q", q=64),
                in1=DT[:, c, :].unsqueeze(2).to_broadcast([128, 8, 64]), op=ALU.mult), ["PS3", "DT"], ["XDT"])
            S.op("act", lambda e, PT=PT: e.copy(BTOK[:].rearrange("p g n -> p (g n)"), PT[:, 512:768]), ["PS3"], ["BTOK"])
            S.op("dve", lambda e: e.tensor_tensor(out=XDD[:], in0=XDT[:],
                                                   in1=DEC[:].unsqueeze(2).to_broadcast([128, 8, 64]), op=ALU.mult),
                 ["XDT", "DEC"], ["XDD"])
            yb = 5 + c % 2

            def mmy(e, yb=yb):
                ins = None
                for h in range(8):
                    i, half = h // 2, h % 2
                    o = PS[half * 64:(half + 1) * 64, yb, i * 128:(i + 1) * 128]
                    e.matmul(o, lhsT=XDT[:, h, :], rhs=MT[:, h, :], start=True, stop=False)
                    ins = e.matmul(o, lhsT=HTB[:, h, :], rhs=CSC[:, h, :], start=False, stop=True)
                return ins
            S.op("pe", mmy, ["XDT", "MT", "HTB", "CSC"], ["PS%d" % yb])
            S.op("act", lambda e, yb=yb, cs=cs: e.copy(YS[:, :, cs], PS[:, yb, :].rearrange("p (i l) -> p i l", l=128)),
                 ["PS%d" % yb], ["YS0", "YS1", "YS2", "YS3"])

            def mms(e):
                e.matmul(PS[:, 4, 0:256], lhsT=BTOK[:, 0, :], rhs=XDD[:, 0:4, :].rearrange("p h q -> p (h q)"),
                         start=True, stop=True)
                return e.matmul(PS[:, 4, 256:512], lhsT=BTOK[:, 1, :], rhs=XDD[:, 4:8, :].rearrange("p h q -> p (h q)"),
                                start=True, stop=True)
            S.op("pe", mms, ["BTOK", "XDD"], ["PS4"])
            S.op("dve", lambda e: e.tensor_tensor(out=HT[:], in0=HT[:],
                                                  in1=EROW[:, :, 127:128].to_broadcast([128, 8, 64]), op=ALU.mult),
                 ["HT", "EROW"], ["HT"])
            S.op("dve", lambda e: e.tensor_tensor(out=HT[:], in0=HT[:],
                                                  in1=PS[:, 4, :].rearrange("p (h q) -> p h q", q=64), op=ALU.add),
                 ["HT", "PS4"], ["HT"])
            S.op("act", lambda e: e.copy(HTB[:], HT[:]), ["HT"], ["HTB"])
        for i in range(4):
            S.op("dve", lambda e, i=i: e.scalar_tensor_tensor(out=YS[:, i, :], in0=XS[:, i, :], scalar=SSDC[:, 0, i:i + 1],
                                                             in1=YS[:, i, :], op0=ALU.mult, op1=ALU.add),
                 ["YS%d" % i, "XS%d" % i, "SSDC"], ["YS%d" % i])
            S.op("dve", lambda e, i=i: e.tensor_tensor(out=YS[:, i, :], in0=YS[:, i, :], in1=SZ[:, i, :], op=ALU.mult),
                 ["YS%d" % i, "SZ%d" % i], ["YS%d" % i])
            S.op("act", lambda e, i=i: e.activation(out=G[:, i, :], in_=YS[:, i, :], func=AF.Square), ["YS%d" % i], [Gk[i]])
        for g in range(2):
            def mmq(e, g=g):
                e.matmul(PS[:, 6 + g, :], lhsT=ONERMS[:], rhs=G[:, 2 * g, :], start=True, stop=False)
                return e.matmul(PS[:, 6 + g, :], lhsT=ONERMS[:], rhs=G[:, 2 * g + 1, :], start=False, stop=True)
            S.op("pe", mmq, [Gk[2 * g], Gk[2 * g + 1], "ONERMS"], ["PS%d" % (6 + g)])
            rs = MEAN if g == 0 else RSTD
            rk = "MEAN" if g == 0 else "RSTD"
            S.op("dve", lambda e, g=g, rs=rs: e.tensor_scalar_add(rs[:], PS[:, 6 + g, :], LN_EPS), ["PS%d" % (6 + g)], [rk])
            S.op("act", lambda e, rs=rs: e.activation(out=rs[:], in_=rs[:], func=AF.Sqrt), [rk], [rk])
            S.op("dve", lambda e, rs=rs: e.reciprocal(rs[:], rs[:]), [rk], [rk])
            for i in (2 * g, 2 * g + 1):
                S.op("dve", lambda e, i=i, rs=rs: e.scalar_tensor_tensor(out=YCAT[:, i, :], in0=YS[:, i, :],
                                                                        scalar=SSDC[:, 1, i:i + 1], in1=rs[:],
                                                                        op0=ALU.mult, op1=ALU.mult),
                     ["YS%d" % i, rk, "SSDC"], ["YC%d" % i])
        S5ALIAS = WREk + WIMk + ["XRF", "XIF", "XIN"]
        S.op("pool", lambda e: e.nop(), [], S5ALIAS)
        for k in range(4):
            ks = slice(k * 128, (k + 1) * 128)
            for hb in range(2):
                g0 = hb * 8
                pr, pi = (0, 1) if (2 * k + hb) % 2 == 0 else (2, 3)

                def mmb(e, g0=g0, ks=ks, pr=pr, pi=pi):
                    ins = None
                    for j in range(8):
                        gp = g0 + j
                        e.matmul(PS[:, pr + j // 4, (j % 4) * 128:(j % 4 + 1) * 128], lhsT=BBR[:, gp, :], rhs=U[:, gp // 4, ks],
                                 start=True, stop=True)
                    return ins
                br_b = 0 if (2 * k + hb) % 2 == 0 else 4

                def mmb2(e, g0=g0, ks=ks, br_b=br_b):
                    ins = None
                    for j in range(8):
                        gp = g0 + j
                        e.matmul(PS[:, br_b + j // 4, (j % 4) * 128:(j % 4 + 1) * 128], lhsT=BBR[:, gp, :],
                                 rhs=U[:, gp // 4, ks], start=True, stop=True)
                        ins = e.matmul(PS[:, br_b + 2 + j // 4, (j % 4) * 128:(j % 4 + 1) * 128], lhsT=BBI[:, gp, :],
                                       rhs=U[:, gp // 4, ks], start=True, stop=True)
                    return ins
                pk = ["PS%d" % (br_b + q) for q in range(4)]
                S.op("pe", mmb2, ["BBR", "BBI", "U0", "U1", "U2", "U3"], pk)
                BRp = PS[:, br_b:br_b + 2, :].rearrange("p a (j t) -> p (a j) t", t=128)
                BIp = PS[:, br_b + 2:br_b + 4, :].rearrange("p a (j t) -> p (a j) t", t=128)
                co = COST[:, g0:g0 + 8, :]
                si_ = SINT[:, g0:g0 + 8, :]
                S.op("dve", lambda e, BRp=BRp, co=co: e.tensor_tensor(out=BPR[:], in0=BRp, in1=co, op=ALU.mult),
                     pk + ["COST"], ["BPR"])
                S.op("dve", lambda e, BIp=BIp, si_=si_: e.tensor_tensor(out=TMPA[:], in0=BIp, in1=si_, op=ALU.mult),
                     pk + ["SINT"], ["RHSA"])
                S.op("dve", lambda e: e.tensor_tensor(out=BPR[:], in0=BPR[:], in1=TMPA[:], op=ALU.add),
                     ["BPR", "RHSA"], ["BPR"])
                S.op("dve", lambda e, BIp=BIp, co=co: e.tensor_tensor(out=BPI[:], in0=BIp, in1=co, op=ALU.mult),
                     pk + ["COST"], ["BPI"])
                S.op("dve", lambda e, BRp=BRp, si_=si_: e.tensor_tensor(out=TMPB[:], in0=BRp, in1=si_, op=ALU.mult),
                     pk + ["SINT"], ["DIFF"])
                S.op("dve", lambda e: e.tensor_tensor(out=BPI[:], in0=BPI[:], in1=TMPB[:], op=ALU.subtract),
                     ["BPI", "DIFF"], ["BPI"])
                for j in range(8):
                    gp = g0 + j
                    S.op("dve", lambda e, j=j, gp=gp: e.tensor_tensor_scan(
                        out=WRE[:, j, :], data0=MAG[:, gp:gp + 1].to_broadcast([128, 128]), data1=BPR[:, j, :],
                        initial=XST[:, 0, gp:gp + 1], op0=ALU.mult, op1=ALU.add), ["BPR", "XST", "S5V"], ["WRE%d" % j])
                    S.op("dve", lambda e, j=j, gp=gp: e.tensor_tensor_scan(
                        out=WIM[:, j, :], data0=MAG[:, gp:gp + 1].to_broadcast([128, 128]), data1=BPI[:, j, :],
                        initial=XST[:, 1, gp:gp + 1], op0=ALU.mult, op1=ALU.add), ["BPI", "XST", "S5V"], ["WIM%d" % j])
                S.op("dve", lambda e, co=co: e.tensor_tensor(out=XRF, in0=WRE, in1=co, op=ALU.mult),
                     WREk + ["COST"], ["XRF"])
                S.op("dve", lambda e, si_=si_: e.tensor_tensor(out=TMPA[:], in0=WIM, in1=si_, op=ALU.mult),
                     WIMk + ["SINT"], ["RHSA"])
                S.op("dve", lambda e: e.tensor_tensor(out=XRF, in0=XRF, in1=TMPA[:], op=ALU.subtract),
                     ["XRF", "RHSA"], ["XRF"])
                S.op("dve", lambda e, co=co: e.tensor_tensor(out=XIF, in0=WIM, in1=co, op=ALU.mult),
                     WIMk + ["COST"], ["XIF"])
                S.op("dve", lambda e, si_=si_: e.tensor_tensor(out=TMPB[:], in0=WRE, in1=si_, op=ALU.mult),
                     WREk + ["SINT"], ["DIFF"])
                S.op("dve", lambda e: e.tensor_tensor(out=XIF, in0=XIF, in1=TMPB[:], op=ALU.add),
                     ["XIF", "DIFF"], ["XIF"])
                S.op("act", lambda e, g0=g0: e.copy(XST[:, 0, g0:g0 + 8], XRF[:, :, 127]), ["XRF"], ["XST"])
                S.op("act", lambda e, g0=g0: e.copy(XST[:, 1, g0:g0 + 8], XIF[:, :, 127]), ["XIF", "XST"], ["XST"])
                S.op("act", lambda e: e.copy(XRB[:], XRF), ["XRF"], ["LMAT"])
                S.op("act", lambda e: e.copy(XIB[:], XIF), ["XIF"], ["MT"])
                for cc in range(2):
                    ct = hb * 2 + cc

                    def mmo(e, cc=cc, ct=ct, g0=g0):
                        ins = None
                        for q in range(4):
                            j = cc * 4 + q
                            gp = g0 + j
                            e.matmul(PS[:, 7, ct * 128:(ct + 1) * 128], lhsT=CRE[:, gp, :], rhs=XRB[:, j, :],
                                     start=(q == 0), stop=False)
                            ins = e.matmul(PS[:, 7, ct * 128:(ct + 1) * 128], lhsT=CIMN[:, gp, :], rhs=XIB[:, j, :],
                                           start=False, stop=(q == 3))
                        return ins
                    S.op("pe", mmo, ["CRE", "CIMN", "LMAT", "MT"], ["PS7"])
                    S.op("dve", lambda e, ct=ct, ks=ks: e.scalar_tensor_tensor(
                        out=YS[:, ct, ks], in0=U[:, ct, ks], scalar=SSDC[:, 2, ct:ct + 1],
                        in1=PS[:, 7, ct * 128:(ct + 1) * 128], op0=ALU.mult, op1=ALU.add),
                        ["PS7", "U%d" % ct, "SSDC"], ["YS%d" % ct])
        S.op("pool", lambda e: e.nop(), [], S5ALIAS)
        for ct in range(4):
            S.op("act", lambda e, ct=ct: e.activation(out=G[:, 4 + ct, :], in_=YS[:, ct, :], func=AF.Gelu_apprx_tanh),
                 ["YS%d" % ct], [Gk[4 + ct]])
        for mt in range(4):
            slot = wload(wglus_d[mt], 512)
            pbk = mt % 2

            def mmg(e, slot=slot, pbk=pbk):
                ins = None
                for kt in range(4):
                    ins = e.matmul(PS[:, pbk, :], lhsT=WR[:, slot, kt * 128:(kt + 1) * 128], rhs=G[:, 4 + kt, :],
                                   start=(kt == 0), stop=(kt == 3))
                return ins
            S.op("pe", mmg, ["WR%d" % slot] + Gk[4:8], ["PS%d" % pbk])
            S.op("act", lambda e, mt=mt, pbk=pbk: e.activation(out=SG[:, pbk, :], in_=PS[:, pbk, :], func=AF.Sigmoid,
                                                               bias=BGLU[:, mt:mt + 1]), ["PS%d" % pbk, "BGLU"],
                 ["SG%d" % pbk])
            S.op("dve", lambda e, mt=mt, pbk=pbk: e.tensor_tensor(out=YCAT[:, 4 + mt, :], in0=G[:, 4 + mt, :],
                                                                 in1=SG[:, pbk, :], op=ALU.mult),
                 [Gk[4 + mt], "SG%d" % pbk], ["YC%d" % (4 + mt)])
        YCk = ["YC%d" % i for i in range(8)]
        for m in range(8):
            slot = wload(wouts_d[m], 1024)
            pb_ = 2 + m % 2

            def mmw(e, slot=slot, pb_=pb_):
                ins = None
                for kt in range(8):
                    ins = e.matmul(PS[:, pb_, :], lhsT=WR[:, slot, kt * 128:(kt + 1) * 128], rhs=YCAT[:, kt, :],
                                   start=(kt == 0), stop=(kt == 7))
                return ins
            S.op("pe", mmw, ["WR%d" % slot] + YCk, ["PS%d" % pb_])
            S.op("dve", lambda e, m=m, pb_=pb_: e.scalar_tensor_tensor(out=R[:, m, :], in0=PS[:, pb_, :],
                                                                     scalar=VB("gt2", b, m), in1=R[:, m, :],
                                                                     op0=ALU.mult, op1=ALU.add),
                 ["PS%d" % pb_, Rk[m], "VECB"], [Rk[m]])

    tiles = [(s, t) for s in range(2) for t in range(NTILE)]
    if limit == "pro":
        tiles = []
    else:
        load_x(0)
    for ti, (s, t) in enumerate(tiles):
        b = s
        row0 = s * S_LEN + t * NT
        for kt in range(8):
            pbk = kt % 4

            def tp(e, kt=kt, pbk=pbk):
                ins = None
                for j in range(4):
                    ins = e.transpose(PS[:, pbk, j * 128:(j + 1) * 128], XIN[:, j, kt * 128:(kt + 1) * 128], IDF)
                return ins
            S.op("pe", tp, ["XIN", "CST"], ["PS%d" % pbk])
            S.op("act", lambda e, kt=kt, pbk=pbk: e.mul(R[:, kt, :], PS[:, pbk, :], ALPHA), ["PS%d" % pbk], [Rk[kt]])
            S.op("act", lambda e, kt=kt, pbk=pbk, b=b: e.activation(out=H[:, kt, :], in_=PS[:, pbk, :], func=AF.Identity,
                                                                   bias=VB("b1", b, kt), scale=VB("a1", b, kt)),
                 ["PS%d" % pbk, "VECB"], [Hk[kt]])
        if limit == "xin":
            break
        ffn(0, b, "gt1")
        if limit == "ffn":
            break
        layer_norm(b, "ag1", "ab1", "gn2", "bn2")
        if limit == "ln1":
            break
        if ti == 0:
            tap("r2", slice(None), R[:].rearrange("p k t -> p (k t)"), Rk)
        mixer(b, first=(t == 0))
        if ti == 0:
            tap("ycat", slice(None), YS[:].rearrange("p k t -> p (k t)"), ["YS%d" % q for q in range(4)])
            if "ycb" in tap_d:
                S.dma("pool", tap_d["ycb"], YCAT[:].rearrange("p k t -> p (k t)"), ["YC%d" % q for q in range(8)],
                      ["tap_ycb"])
        if limit == "mix":
            break
        layer_norm(b, "ag2", "ab2", "gn3", "bn3")
        if ti == 0:
            tap("r3", slice(None), R[:].rearrange("p k t -> p (k t)"), Rk)
        if ti + 1 < len(tiles):
            s2, t2 = tiles[ti + 1]
            load_x(s2 * S_LEN + t2 * NT)
        ffn(1, b, "gt3")
        layer_norm(b, "ln3g", "ln3b", None, None, final=True)
        OUTS = G[:].rearrange("p f t -> p (f t)").bitcast(F32)[:, 0:4096].rearrange("p (j d) -> p j d", d=1024)
        for j in range(4):
            for hf in range(2):
                pbk = (2 * j + hf) % 4

                def tpo(e, j=j, hf=hf, pbk=pbk):
                    ins = None
                    for q in range(4):
                        kt = hf * 4 + q
                        ins = e.transpose(PS[:, pbk, q * 128:(q + 1) * 128], R[:, kt, j * 128:(j + 1) * 128], IDF)
                    return ins
                S.op("pe", tpo, Rk + ["CST"], ["PS%d" % pbk])
                if hf == 0:
                    S.op("act", lambda e, j=j, pbk=pbk: e.copy(OUTS[:, j, 0:512], PS[:, pbk, :]), ["PS%d" % pbk],
                         Gk[0:16])
                else:
                    S.op("dve", lambda e, j=j, pbk=pbk: e.tensor_copy(OUTS[:, j, 512:1024], PS[:, pbk, :]),
                         ["PS%d" % pbk], Gk[0:16])
        S.dma("act", out_d[row0:row0 + NT, :].rearrange("(j p) d -> p j d", p=128), OUTS, Gk[0:16], ["OUT"],
              sem="out")

    S.emit(nc, ["OUT"] + ["tap_" + nm for nm in tap_d])
    es.close()
    return nc


def _fm(v, nt):
    return np.ascontiguousarray(np.asarray(v, np.float32).reshape(nt, 128).T)


def make_shared(inp):
    f = np.float32
    g = lambda k: np.asarray(inp[k], f)[0]
    sh = {}
    sh["w_ada"] = np.ascontiguousarray(g("w_ada"))
    sh["b_ada"] = _fm(g("b_ada"), 72)
    for k in ["ffn1_w1", "ffn1_w3", "ffn1_w2", "ffn2_w1", "ffn2_w3", "ffn2_w2", "w_in", "w_glu", "w_out"]:
        sh[k] = np.ascontiguousarray(g(k))
    sh["vecs"] = np.ascontiguousarray(np.stack(
        [_fm(g(k), 8) for k in ["ln1_g", "ln1_b", "ln2_g", "ln2_b", "ln3_g", "ln3_b", "conv_b"]], 1))
    sh["convw"] = np.ascontiguousarray(g("conv_w").T.reshape(8, 128, 4).transpose(1, 0, 2))
    ssdv = np.zeros((128, 3, 8), f)
    ssdv[:, 0, :] = g("dt_bias")[None, :]
    ssdv[:, 1, :] = g("a_log")[None, :]
    sh["ssdv"] = ssdv
    ssdc = np.zeros((128, 3, 4), f)
    ssdc[:, 0, :] = _fm(np.repeat(g("d_ssd"), 64), 4)
    ssdc[:, 1, :] = _fm(g("ssd_norm_w"), 4)
    ssdc[:, 2, :] = _fm(g("s5_d"), 4)
    sh["ssdc"] = ssdc
    sh["bglu"] = _fm(g("b_glu"), 4)
    def st(a):
        return np.ascontiguousarray(a.reshape(16, 2, 64).transpose(1, 2, 0).reshape(128, 16))
    s5v = np.zeros((128, 3, 16), f)
    s5v[:, 0, :] = st(g("s5_a_re"))
    s5v[:, 1, :] = st(g("s5_a_im"))
    s5v[:, 2, :] = st(np.repeat(g("s5_log_dt")[:, None], 64, 1))
    sh["s5v"] = s5v
    bt = np.zeros((128, 2, 16, 128), f)
    ctt = np.zeros((128, 2, 16, 128), f)
    bre, bim = g("s5_b_re"), g("s5_b_im")
    cre, cim = g("s5_c_re"), g("s5_c_im")
    for gp in range(16):
        for g2 in range(2):
            grp = 2 * gp + g2
            k0 = (grp % 8) * 16
            bt[g2 * 64:(g2 + 1) * 64, 0, gp, k0:k0 + 16] = bre[grp]
            bt[g2 * 64:(g2 + 1) * 64, 1, gp, k0:k0 + 16] = bim[grp]
            ctt[g2 * 64:(g2 + 1) * 64, 0, gp, k0:k0 + 16] = cre[grp].T
            ctt[g2 * 64:(g2 + 1) * 64, 1, gp, k0:k0 + 16] = cim[grp].T
    sh["s5bt"] = bt
    sh["s5ct"] = ctt
    cst = np.zeros((128, 3, 128), f)
    cst[:, 0, :] = np.eye(128, dtype=f)
    cst[:, 1, :] = np.triu(np.ones((128, 128), f))
    cst[:, 2, :] = np.arange(1, 129, dtype=f)[None, :]
    sh["cst"] = cst
    return sh


def make_core(inp, sh, b0, S_LEN):
    m = dict(sh)
    x = np.asarray(inp["x"], np.float32)
    m["x"] = np.ascontiguousarray(x[b0:b0 + 2].reshape(2 * S_LEN, 1024))
    c = np.asarray(inp["c"], np.float32)[b0:b0 + 2]
    m["cT"] = np.ascontiguousarray(c.T.reshape(8, 128, 2).transpose(1, 0, 2))
    return m


_NC_CACHE = {}


def kernel(**inputs):
    x = np.asarray(inputs["x"])
    B, S_LEN, D = x.shape
    n = B // 2
    if S_LEN not in _NC_CACHE:
        nc = bass.Bass("TRN2", target_bir_lowering=False)
        build(nc, S_LEN)
        _NC_CACHE[S_LEN] = nc
    nc = _NC_CACHE[S_LEN]
    sh = make_shared(inputs)
    in_maps = [make_core(inputs, sh, 2 * i, S_LEN) for i in range(n)]
    res = run_bass_kernel_spmd(nc, in_maps, core_ids=list(range(n)))
    outs = [r["out"].reshape(2, S_LEN, D) for r in res.results]
    return np.concatenate(outs, 0).astype(np.float32)
```
